# Optimizing a Trainium2 kernel written in Bass

```python
import jax, jax.numpy as jnp
from jax import lax
import numpy as np

D_MODEL = 2048
BATCH = 2
SEQ = 8192
DEPTH = 1

GRID_W = 64
CTX_LEN = 256
HEAD_DIM = 128
N_Q_HEADS = 16
N_KV_HEADS = 4
Q_PER_KV = N_Q_HEADS // N_KV_HEADS
WINDOW = 128
BLOCK = 128
ROPE_THETA = 10000.0
GLA_HEADS = 4
GLA_DK = D_MODEL // 2 // GLA_HEADS
GLA_DV = D_MODEL // GLA_HEADS
GLA_LOWRANK = 16
GLA_GATE_NORM = 16.0
GLA_CHUNK = 64
D_FF = 5632
CONV_W = 3
EPS = 1e-6
ATTN_WIDTH = N_Q_HEADS * HEAD_DIM
KV_WIDTH = N_KV_HEADS * HEAD_DIM
GLA_K_WIDTH = GLA_HEADS * GLA_DK
GLA_V_WIDTH = GLA_HEADS * GLA_DV
IN_SPLITS = (ATTN_WIDTH, KV_WIDTH, KV_WIDTH, GLA_K_WIDTH, GLA_K_WIDTH, GLA_V_WIDTH, GLA_V_WIDTH,
             GLA_LOWRANK, GLA_LOWRANK, D_MODEL, D_MODEL)
IN_WIDTH = sum(IN_SPLITS)

kernel_name = "hybrid_swa_gla_convffn_prefix_dit"


def rms_norm(x, g):
    xf = x.astype(jnp.float32)
    y = xf * lax.rsqrt(jnp.mean(xf * xf, axis=-1, keepdims=True) + EPS)
    return (y * g.astype(jnp.float32)).astype(x.dtype)


def modulate(x, g, shift, scale):
    return rms_norm(x, g) * (1 + scale) + shift


def split_heads(a, n_heads):
    return a.reshape(*a.shape[:-1], n_heads, -1)


def axial_rope_angles(n):
    rows = n // GRID_W
    row = jnp.repeat(jnp.arange(rows), GRID_W)
    col = jnp.tile(jnp.arange(GRID_W), rows)
    n_freq = HEAD_DIM // 4
    inv = ROPE_THETA ** (-jnp.arange(n_freq, dtype=jnp.float32) / n_freq)
    ang = jnp.concatenate([row[:, None] * inv, col[:, None] * inv], axis=-1)
    return jnp.cos(ang), jnp.sin(ang)


def apply_rope(x, cos, sin):
    half = HEAD_DIM // 2
    x1, x2 = x[..., :half].astype(jnp.float32), x[..., half:].astype(jnp.float32)
    c, s = cos[None, :, None, :], sin[None, :, None, :]
    return jnp.concatenate([x1 * c - x2 * s, x2 * c + x1 * s], axis=-1).astype(x.dtype)


def gqa_scores(q5, k):
    return jnp.einsum("bqhgd,bkhd->bhgqk", q5, k).astype(jnp.float32) * (HEAD_DIM ** -0.5)


def attend_with_sink(parts, sink_g):
    sink_b = sink_g[None, :, :, None]
    m = sink_b
    for s, _ in parts:
        m = jnp.maximum(m, s.max(axis=-1))
    denom = jnp.exp(sink_b - m)
    out = None
    for s, val in parts:
        p = jnp.exp(s - m[..., None])
        denom = denom + p.sum(axis=-1)
        o = jnp.einsum("bhgqk,bkhd->bqhgd", p, val.astype(jnp.float32))
        out = o if out is None else out + o
    return out / jnp.transpose(denom, (0, 3, 1, 2))[..., None]


def windowed_attention(q, k, v, k_ctx, v_ctx, sink):
    B, n = q.shape[:2]
    nb = n // BLOCK
    span = BLOCK + 2 * WINDOW
    pad = ((0, 0), (WINDOW, WINDOW), (0, 0), (0, 0))
    kp, vp = jnp.pad(k, pad), jnp.pad(v, pad)
    qb = jnp.swapaxes(q.reshape(B, nb, BLOCK, N_KV_HEADS, Q_PER_KV, HEAD_DIM), 0, 1)
    sink_g = sink.reshape(N_KV_HEADS, Q_PER_KV).astype(jnp.float32)
    s_ctx_all = None

    def block(args):
        qi, i = args
        start = i * BLOCK
        kw = lax.dynamic_slice_in_dim(kp, start, span, axis=1)
        vw = lax.dynamic_slice_in_dim(vp, start, span, axis=1)
        qpos = start + jnp.arange(BLOCK)
        kpos = start - WINDOW + jnp.arange(span)
        valid = ((jnp.abs(qpos[:, None] - kpos[None, :]) <= WINDOW)
                 & (kpos >= 0)[None, :] & (kpos < n)[None, :])
        s_lat = jnp.where(valid, gqa_scores(qi, kw), -jnp.inf)
        s_ctx = gqa_scores(qi, k_ctx)
        return attend_with_sink([(s_lat, vw), (s_ctx, v_ctx)], sink_g)

    o = lax.map(block, (qb, jnp.arange(nb)))
    return jnp.swapaxes(o, 0, 1).reshape(B, n, ATTN_WIDTH).astype(q.dtype)


def context_attention(q, k, v, sink):
    B, L = q.shape[:2]
    q5 = q.reshape(B, L, N_KV_HEADS, Q_PER_KV, HEAD_DIM)
    sink_g = sink.reshape(N_KV_HEADS, Q_PER_KV).astype(jnp.float32)
    o = attend_with_sink([(gqa_scores(q5, k), v)], sink_g)
    return o.reshape(B, L, ATTN_WIDTH).astype(q.dtype)


def gla_chunked(q, k, v, g, s0):
    B, T, H, _ = q.shape
    dv = v.shape[-1]
    nc = T // GLA_CHUNK

    def to_chunks(a):
        return jnp.transpose(a.reshape(B, nc, GLA_CHUNK, H, a.shape[-1]), (1, 0, 3, 2, 4)).astype(jnp.float32)

    causal = jnp.tril(jnp.ones((GLA_CHUNK, GLA_CHUNK), dtype=bool))[..., None]

    def step(S, inp):
        qc, kc, vc, gc = inp
        b = jnp.cumsum(gc, axis=2)
        b_last = b[:, :, -1:]
        o_inter = jnp.einsum("bhcd,bhde->bhce", qc * jnp.exp(b), S)
        rel = jnp.where(causal, b[:, :, :, None, :] - b[:, :, None, :, :], -jnp.inf)
        A = jnp.einsum("bhtd,bhsd,bhtsd->bhts", qc, kc, jnp.exp(rel))
        o = o_inter + jnp.einsum("bhts,bhse->bhte", A, vc)
        S = (jnp.exp(b_last[:, :, 0])[..., None] * S
             + jnp.einsum("bhsd,bhse->bhde", kc * jnp.exp(b_last - b), vc))
        return S, o

    S, o = lax.scan(step, s0, (to_chunks(q), to_chunks(k), to_chunks(v), to_chunks(g)))
    o = jnp.transpose(o, (1, 0, 3, 2, 4)).reshape(B, T, H, dv)
    return o.astype(v.dtype), S


def gla_final_state(k, v, g):
    b = jnp.cumsum(g.astype(jnp.float32), axis=1)
    w = jnp.exp(b[:, -1:] - b)
    return jnp.einsum("bthd,bthe->bhde", k.astype(jnp.float32) * w, v.astype(jnp.float32))


def flip(a):
    return a[:, ::-1]


def context_gla(q, k, v, gf, gb, need_out):
    if need_out:
        s0 = jnp.zeros((q.shape[0], GLA_HEADS, GLA_DK, GLA_DV), jnp.float32)
        of, sf = gla_chunked(q, k, v, gf, s0)
        ob, sb = gla_chunked(flip(q), flip(k), flip(v), flip(gb), s0)
        return sf, sb, of + flip(ob)
    return gla_final_state(k, v, gf), gla_final_state(flip(k), flip(v), flip(gb)), None


def latent_gla(q, k, v, gf, gb, sf, sb):
    of, _ = gla_chunked(q, k, v, gf, sf)
    ob, _ = gla_chunked(flip(q), flip(k), flip(v), flip(gb), sb)
    return of + flip(ob)


def project_heads(h, w_in, q_norm, k_norm, w_gate_f, b_gate_f, w_gate_b, b_gate_b):
    offs = np.cumsum(IN_SPLITS)[:-1].tolist()
    qa, ka, va, qb, kb, vb, rb, lrf, lrb, gate_a, gate_b = jnp.split(h @ w_in, offs, axis=-1)
    qa = rms_norm(split_heads(qa, N_Q_HEADS), q_norm)
    ka = rms_norm(split_heads(ka, N_KV_HEADS), k_norm)
    va = split_heads(va, N_KV_HEADS)
    qb = split_heads(qb, GLA_HEADS) * (GLA_DK ** -0.5)
    kb = split_heads(kb, GLA_HEADS)
    vb = split_heads(vb, GLA_HEADS)
    gf = split_heads(jax.nn.log_sigmoid((lrf @ w_gate_f + b_gate_f).astype(jnp.float32)) / GLA_GATE_NORM, GLA_HEADS)
    gb = split_heads(jax.nn.log_sigmoid((lrb @ w_gate_b + b_gate_b).astype(jnp.float32)) / GLA_GATE_NORM, GLA_HEADS)
    return qa, ka, va, qb, kb, vb, rb, gf, gb, gate_a, gate_b


def merge_branches(o_attn, o_gla, rb, gate_a, gate_b, gla_norm, w_attn_o, w_gla_o, w_out):
    B, T = o_attn.shape[:2]
    y_att = o_attn @ w_attn_o
    y_gla = (rms_norm(o_gla, gla_norm).reshape(B, T, GLA_V_WIDTH) * jax.nn.silu(rb)) @ w_gla_o
    return (jax.nn.sigmoid(gate_a) * y_att + jax.nn.sigmoid(gate_b) * y_gla) @ w_out


def conv_ffn(h, w_up, conv_w, conv_b, w_down):
    T = h.shape[1]
    u = h @ w_up
    up = jnp.pad(u, ((0, 0), (CONV_W // 2, CONV_W // 2), (0, 0)))
    u = sum(up[:, j:j + T] * conv_w[j] for j in range(CONV_W)) + conv_b
    a, g = jnp.split(u, 2, axis=-1)
    return (jax.nn.silu(a) * g) @ w_down


def setup_inputs(seed: int = 0) -> dict:
    key = jax.random.key(seed)
    ks = jax.random.split(key, 24)
    D, L = D_MODEL, DEPTH

    def nrm(k, shape, s):
        return jax.random.normal(k, shape, jnp.float32) * s

    return {
        "x": nrm(ks[0], (BATCH, SEQ, D), 1.0),
        "c": nrm(ks[1], (BATCH, D), 1.0),
        "ctx": nrm(ks[2], (BATCH, CTX_LEN, D), 1.0),
        "c_ctx": nrm(ks[3], (D,), 1.0),
        "w_mod": nrm(ks[4], (L, D, 6 * D), 0.5 * D ** -0.5),
        "b_mod": nrm(ks[5], (L, 6 * D), 0.02),
        "g_mix": 1.0 + nrm(ks[6], (L, D), 0.05),
        "w_in": nrm(ks[7], (L, D, IN_WIDTH), D ** -0.5),
        "q_norm": 1.0 + nrm(ks[8], (L, HEAD_DIM), 0.05),
        "k_norm": 1.0 + nrm(ks[9], (L, HEAD_DIM), 0.05),
        "attn_sink": nrm(ks[10], (L, N_Q_HEADS), 0.5),
        "w_gate_f": nrm(ks[11], (L, GLA_LOWRANK, GLA_K_WIDTH), GLA_LOWRANK ** -0.5),
        "b_gate_f": nrm(ks[12], (L, GLA_K_WIDTH), 0.1),
        "w_gate_b": nrm(ks[13], (L, GLA_LOWRANK, GLA_K_WIDTH), GLA_LOWRANK ** -0.5),
        "b_gate_b": nrm(ks[14], (L, GLA_K_WIDTH), 0.1),
        "gla_norm": 1.0 + nrm(ks[15], (L, GLA_DV), 0.05),
        "w_attn_o": nrm(ks[16], (L, ATTN_WIDTH, D), ATTN_WIDTH ** -0.5),
        "w_gla_o": nrm(ks[17], (L, GLA_V_WIDTH, D), GLA_V_WIDTH ** -0.5),
        "w_out": nrm(ks[18], (L, D, D), D ** -0.5),
        "g_ffn": 1.0 + nrm(ks[19], (L, D), 0.05),
        "w_up": nrm(ks[20], (L, D, 2 * D_FF), D ** -0.5),
        "conv_w": nrm(ks[21], (L, CONV_W, 2 * D_FF), CONV_W ** -0.5),
        "conv_b": nrm(ks[22], (L, 2 * D_FF), 0.02),
        "w_down": nrm(ks[23], (L, D_FF, D), D_FF ** -0.5),
    }


def reference(x, c, ctx, c_ctx, w_mod, b_mod, g_mix, w_in, q_norm, k_norm, attn_sink,
              w_gate_f, b_gate_f, w_gate_b, b_gate_b, gla_norm, w_attn_o, w_gla_o, w_out,
              g_ffn, w_up, conv_w, conv_b, w_down):
    n = x.shape[1]
    cos, sin = axial_rope_angles(n)
    for l in range(DEPTH):
        last = l == DEPTH - 1
        mod_x = jnp.split((jax.nn.silu(c) @ w_mod[l] + b_mod[l])[:, None, :], 6, axis=-1)
        mod_c = jnp.split((jax.nn.silu(c_ctx) @ w_mod[l] + b_mod[l])[None, None, :], 6, axis=-1)
        proj = lambda h: project_heads(h, w_in[l], q_norm[l], k_norm[l],
                                       w_gate_f[l], b_gate_f[l], w_gate_b[l], b_gate_b[l])

        qa, ka, va, qb, kb, vb, rb, gf, gb, gate_a, gate_b = proj(modulate(x, g_mix[l], mod_x[0], mod_x[1]))
        qa, ka = apply_rope(qa, cos, sin), apply_rope(ka, cos, sin)
        cqa, cka, cva, cqb, ckb, cvb, crb, cgf, cgb, cgate_a, cgate_b = proj(
            modulate(ctx, g_mix[l], mod_c[0], mod_c[1]))

        o_attn = windowed_attention(qa, ka, va, cka, cva, attn_sink[l])
        sf, sb, o_gla_c = context_gla(cqb, ckb, cvb, cgf, cgb, need_out=not last)
        o_gla = latent_gla(qb, kb, vb, gf, gb, sf, sb)
        x = x + mod_x[2] * merge_branches(o_attn, o_gla, rb, gate_a, gate_b, gla_norm[l],
                                          w_attn_o[l], w_gla_o[l], w_out[l])
        x = x + mod_x[5] * conv_ffn(modulate(x, g_ffn[l], mod_x[3], mod_x[4]),
                                    w_up[l], conv_w[l], conv_b[l], w_down[l])

        if not last:
            o_attn_c = context_attention(cqa, cka, cva, attn_sink[l])
            ctx = ctx + mod_c[2] * merge_branches(o_attn_c, o_gla_c, crb, cgate_a, cgate_b, gla_norm[l],
                                                  w_attn_o[l], w_gla_o[l], w_out[l])
            ctx = ctx + mod_c[5] * conv_ffn(modulate(ctx, g_ffn[l], mod_c[3], mod_c[4]),
                                            w_up[l], conv_w[l], conv_b[l], w_down[l])
    return x
```

```python
import numpy as np
import ml_dtypes
import concourse.bass as bass
import concourse.mybir as mybir
from concourse.bass_utils import run_bass_kernel_spmd

F32 = mybir.dt.float32
BF16 = mybir.dt.bfloat16
AF = mybir.ActivationFunctionType
ALU = mybir.AluOpType

SAME_ENG_GAP = 3
EPS = 1e-6
D = 2048
NDC = 16
DFF = 5632
NFC = 44
CTX = 256


class Buf:
    __slots__ = ("name", "w", "r")

    def __init__(self, name=""):
        self.name = name
        self.w = None
        self.r = []


class Op:
    __slots__ = ("eng", "fn", "deps", "signal", "is_dma", "val", "sem", "pos", "barrier", "is_cc")

    def __init__(self, eng, fn, is_dma):
        self.eng = eng
        self.fn = fn
        self.deps = []
        self.signal = False
        self.is_dma = is_dma
        self.val = None
        self.sem = None
        self.pos = 0
        self.barrier = False
        self.is_cc = False


class Prog:
    ENGS = ("pe", "act", "dve", "pool", "sp")

    def __init__(self, nc):
        self.nc = nc
        self.ops = []
        self.h = {"pe": nc.tensor, "act": nc.scalar, "dve": nc.vector, "pool": nc.gpsimd, "sp": nc.sync}
        self.eng_pos = {e: 0 for e in self.ENGS}
        self.last = {e: None for e in self.ENGS}
        self.bufs = {}
        self.epoch = 0

    def B(self, *key):
        b = self.bufs.get(key)
        if b is None:
            b = Buf(str(key))
            self.bufs[key] = b
        return b

    def op(self, eng, fn, reads=(), writes=(), dma=False, cc=False):
        import os
        if cc and str(cc if cc is not True else 9) in os.environ.get("NOCC", ""):
            return None
        if cc and os.environ.get("NOCC") == "all":
            return None
        i = len(self.ops)
        o = Op(eng, fn, dma)
        o.is_cc = bool(cc)
        o.pos = self.eng_pos[eng]
        self.eng_pos[eng] += 1
        deps = set()
        for b in reads:
            if b.w is not None:
                deps.add(b.w)
        for b in writes:
            if b.w is not None:
                deps.add(b.w)
            for r in b.r:
                deps.add(r)
        for b in reads:
            b.r.append(i)
        for b in writes:
            b.w = i
            b.r = []
        best = {}
        out = []
        for d in deps:
            if d < self.epoch or d == i:
                continue
            od = self.ops[d]
            if od.is_dma or od.is_cc:
                out.append(d)
                continue
            if od.eng == eng and not dma:
                if eng == "pe":
                    continue
                if o.pos - od.pos > SAME_ENG_GAP:
                    continue
            if od.eng not in best or best[od.eng] < d:
                best[od.eng] = d
        out.extend(best.values())
        for d in out:
            self.ops[d].signal = True
        o.deps = out
        self.ops.append(o)
        if not dma and not cc:
            self.last[eng] = i
        return i

    def dma(self, eng, out, in_, reads=(), writes=(), **kw):
        h = self.h[eng]
        return self.op(eng, lambda: h.dma_start(out=out, in_=in_, **kw), reads, writes, dma=True)

    def barrier(self):
        for e in self.ENGS:
            if self.last[e] is not None:
                self.ops[self.last[e]].signal = True
        o = Op("sp", None, False)
        o.barrier = True
        self.ops.append(o)
        self.epoch = len(self.ops)

    def emit(self, final_wait_ops=()):
        nc = self.nc
        sems = {e: nc.alloc_semaphore(name="s_" + e) for e in self.ENGS}
        NDS = {"sp": 12, "pool": 8, "act": 4, "pe": 1, "dve": 1}
        dsems = {e: [nc.alloc_semaphore(name="d_%s_%d" % (e, k)) for k in range(n)] for e, n in NDS.items()}
        duse = {e: [0] * n for e, n in NDS.items()}
        drr = {e: 0 for e in NDS}
        cnt = {e: 0 for e in self.ENGS}
        known = {e: {} for e in self.ENGS}
        nwait = 0
        ccsem = nc.alloc_semaphore(name="cc_sem")
        ccnt = 0

        def wait(e, s, v):
            nonlocal nwait
            kn = known[e]
            if v and kn.get(id(s), 0) < v:
                self.h[e].wait_ge(s, v)
                kn[id(s)] = v
                nwait += 1

        for o in self.ops:
            if o.barrier:
                for e in self.ENGS:
                    for e2 in self.ENGS:
                        if e2 != e:
                            wait(e, sems[e2], cnt[e2])
                    for e2 in NDS:
                        for k in range(NDS[e2]):
                            wait(e, dsems[e2][k], duse[e2][k] * 16)
                    wait(e, ccsem, ccnt)
                continue
            e = o.eng
            for d in o.deps:
                od = self.ops[d]
                wait(e, od.sem, od.val)
            if o.is_cc:
                ccnt += 1
                o.sem = ccsem
                o.val = ccnt
                o.fn().then_inc(ccsem, 1)
            elif o.is_dma:
                k = drr[e]
                drr[e] = (k + 1) % NDS[e]
                s = dsems[e][k]
                wait(e, s, duse[e][k] * 16)
                duse[e][k] += 1
                o.sem = s
                o.val = duse[e][k] * 16
                o.fn().then_inc(s, 16)
            else:
                ins = o.fn()
                if o.signal:
                    cnt[e] += 1
                    o.sem = sems[e]
                    o.val = cnt[e]
                    ins.then_inc(sems[e], 1)
        for d in final_wait_ops:
            od = self.ops[d]
            wait("sp", od.sem, od.val)
        self.stats = dict(n_ops=len(self.ops), n_wait=nwait, cnt=dict(cnt), dmax={e: max(duse[e]) * 16 for e in NDS}, cc=ccnt)


class KB:
    def __init__(self, T, dbg=(), stop_after=99, gather=False, ncores=8):
        self.gather = gather
        self.ncores = ncores
        self.gath = []
        self.T = T
        self.TT = T + 512
        self.NT = T // 512
        self.NBK = T // 128
        self.HL, self.HR, self.CX = T, T + 128, T + 256
        self.dbg = set(dbg)
        self.stop_after = stop_after
        self.nc = bass.Bass("TRN2", target_bir_lowering=False)
        self.P = Prog(self.nc)
        self.sb_off = 16640
        self.sb_base = 16640
        self.uid = 0
        self.outs = []
        self.dram = {}

    def din(self, name, shape, dt=F32):
        t = self.nc.dram_tensor(name, list(shape), dt, kind="ExternalInput")
        self.dram[name] = t
        return t.ap()

    WSPEC = [("wmod", [24, 128, NDC, 512]), ("wfm", [21, 128, NDC, 512]), ("wtm", [7, 128, NDC, 512]),
             ("wao", [8, 128, NDC, 256]), ("wgo", [8, 128, NDC, 256]), ("wo", [8, 128, NDC, 256]),
             ("wup", [2, 22, 128, NDC, 256]), ("wdn", [8, 128, NFC, 256])]
    BLOB_C = 8192

    @classmethod
    def blob_layout(cls):
        offs = {}
        tot = [0, 0]
        for i, (nm, shp) in enumerate(cls.WSPEC):
            bi = 0 if i < 3 else 1
            n = 1
            for s_ in shp:
                n *= s_
            offs[nm] = (bi, tot[bi], n)
            tot[bi] += n
        R = [-(-t // (8 * cls.BLOB_C)) for t in tot]
        return offs, R

    def wsrc(self, name, shape):
        if self.gather == "dry":
            if not hasattr(self, "wdummy"):
                self.wdummy = self.nc.dram_tensor("wdummy", [24 * 128 * NDC * 512], F32)
            n = 1
            for s_ in shape:
                n *= s_
            flat = self.wdummy.ap()[0:n]
            if len(shape) == 5:
                return flat.rearrange("(a g p c n) -> a g p c n", a=shape[0], g=shape[1], p=128, c=shape[3], n=shape[4])
            return flat.rearrange("(g p c n) -> g p c n", g=shape[0], p=128, c=shape[2], n=shape[3])
        if not self.gather:
            return self.din(name, shape)
        if not self.gath:
            offs, R = self.blob_layout()
            for bi in range(2):
                shard = self.din("wblob%d" % bi, [R[bi], self.BLOB_C])
                gi = self.nc.dram_tensor("wblob_gi%d" % bi, [R[bi], self.BLOB_C], F32)
                go = self.nc.dram_tensor("wblob_go%d" % bi, [8 * R[bi], self.BLOB_C], F32)
                self.gath.append((shard, gi, go))
            self.blob_offs = offs
        bi, off, n = self.blob_offs[name]
        go = self.gath[bi][2]
        flat = go.ap().rearrange("r c -> (r c)")[off:off + n]
        if len(shape) == 5:
            return flat.rearrange("(a g p c n) -> a g p c n", a=shape[0], g=shape[1], p=128, c=shape[3], n=shape[4])
        return flat.rearrange("(g p c n) -> g p c n", g=shape[0], p=128, c=shape[2], n=shape[3])

    def dscr(self, name, shape, dt=F32):
        if name in self.dbg:
            t = self.nc.dram_tensor(name, list(shape), dt, kind="ExternalOutput")
        else:
            t = self.nc.dram_tensor(name, list(shape), dt)
        self.dram[name] = t
        return t.ap()

    def sb(self, shape, dt=F32, name=None):
        self.uid += 1
        nm = "%s_%d" % (name or "t", self.uid)
        esz = 4 if dt == F32 else 2
        n = 1
        for s in shape[1:]:
            n *= s
        nbytes = (n * esz + 63) // 64 * 64
        off = self.sb_off
        self.sb_off += nbytes
        assert self.sb_off <= 229376, "SBUF overflow %d in %s" % (self.sb_off, nm)
        t = self.nc.alloc_sbuf_tensor_at(nm, list(shape), dt, offset=off)
        return t.ap()

    def phase_reset(self):
        self.P.barrier()
        self.sb_off = self.sb_base

    def nb(self, name="b"):
        self.uid += 1
        return self.P.B(name, self.uid)

    def mm(self, out, lhsT, rhs, start, stop, reads, writes):
        nc = self.nc
        self.P.op("pe", lambda: nc.tensor.matmul(out, lhsT, rhs, start=start, stop=stop), reads, writes)

    def act(self, out, in_, func, reads, writes, bias=None, scale=None):
        nc = self.nc
        kw = {}
        if bias is not None:
            kw["bias"] = bias
        if scale is not None:
            kw["scale"] = scale
        self.P.op("act", lambda: nc.scalar.activation(out=out, in_=in_, func=func, **kw), reads, writes)

    def tt(self, out, in0, in1, op, reads, writes, eng="dve"):
        h = self.P.h[eng]
        self.P.op(eng, lambda: h.tensor_tensor(out, in0, in1, op), reads, writes)

    def ts(self, out, in0, s1, s2, op0, op1, reads, writes, eng="dve"):
        h = self.P.h[eng]
        if op1 is None:
            self.P.op(eng, lambda: h.tensor_scalar(out, in0, s1, None, op0), reads, writes)
        else:
            self.P.op(eng, lambda: h.tensor_scalar(out, in0, s1, s2, op0, op1), reads, writes)

    def stt(self, out, in0, scalar, in1, op0, op1, reads, writes, eng="dve"):
        h = self.P.h[eng]
        self.P.op(eng, lambda: h.scalar_tensor_tensor(out, in0, scalar, in1, op0, op1), reads, writes)

    def cp(self, out, in_, reads, writes, eng="dve"):
        h = self.P.h[eng]
        self.P.op(eng, lambda: h.tensor_copy(out, in_), reads, writes)

    def recip(self, out, in_, reads, writes):
        nc = self.nc
        self.P.op("dve", lambda: nc.vector.reciprocal(out, in_), reads, writes)

    def memset(self, ap, val, writes, eng="dve"):
        h = self.P.h[eng]
        self.P.op(eng, lambda: h.memset(ap, val), (), writes)

    def ld(self, out, in_, writes, reads=(), eng="sp"):
        return self.P.dma(eng, out, in_, reads=reads, writes=writes)

    def st(self, out, in_, reads, writes=(), eng="sp"):
        return self.P.dma(eng, out, in_, reads=reads, writes=writes)

    def rstd(self, out, in_ps, reads, writes, tmp, tmpb):
        self.act(tmp, in_ps, AF.Ln, reads, [tmpb], bias=self.epscol)
        self.act(out, tmp, AF.Exp, [tmpb], writes, scale=-0.5)

    class WS:
        def __init__(self, kb, groups, shape, nbuf=2, name="w"):
            self.kb = kb
            self.groups = groups
            self.tiles = [kb.sb(shape, BF16, name) for _ in range(nbuf)]
            self.bufs = [kb.nb(name) for _ in range(nbuf)]
            self.issued = 0
            self.nbuf = nbuf

        def get(self, i):
            kb = self.kb
            while self.issued < len(self.groups) and self.issued <= i + self.nbuf - 1:
                k = self.issued
                kb.P.dma("pool", self.tiles[k % self.nbuf], self.groups[k], writes=[self.bufs[k % self.nbuf]])
                self.issued += 1
            return self.tiles[i % self.nbuf], self.bufs[i % self.nbuf]

    def build(self):
        nc, P, T, TT = self.nc, self.P, self.T, self.TT
        NT, NBK = self.NT, self.NBK
        HL, HR, CX = self.HL, self.HR, self.CX
        xe = self.din("xe", [128, NDC, TT])
        cc = self.din("cc", [128, NDC, 2])
        ropeC = self.din("ropeC", [128, TT])
        ropeS = self.din("ropeS", [128, TT])
        pcm = self.din("pcm", [128, 2, 512])
        sel = self.din("sel", [128, 32])
        cst = self.din("cst", [128, 9, 128])
        wmod = self.wsrc("wmod", [24, 128, NDC, 512])
        bmod = self.din("bmod", [128, 96])
        wfm = self.wsrc("wfm", [21, 128, NDC, 512])
        wlr = self.din("wlr", [128, NDC, 32])
        wtm = self.wsrc("wtm", [7, 128, NDC, 512])
        wao = self.wsrc("wao", [8, 128, NDC, 256])
        wgo = self.wsrc("wgo", [8, 128, NDC, 256])
        wo = self.wsrc("wo", [8, 128, NDC, 256])
        wup = self.wsrc("wup", [2, 22, 128, NDC, 256])
        wdn = self.wsrc("wdn", [8, 128, NFC, 256])
        smallc = self.din("smallc", [128, 16 + 16 + 1 + 1 + 4 + 16])
        wg = self.din("wg", [2, 16, 1024])
        bg = self.din("bg", [2, 1, 1024])
        convw = self.din("convw", [128, 88, 3])
        convb = self.din("convb", [128, 88])
        outT = self.nc.dram_tensor("outT", [128, NDC, T], F32, kind="ExternalOutput").ap()

        qT = self.dscr("qT", [128, 16, T], BF16)
        kT = self.dscr("kT", [128, 4, TT], BF16)
        vA = self.dscr("vA", [128, TT // 128, 512], BF16)
        gqT = self.dscr("gqT", [128, 8, TT], BF16)
        gkT = self.dscr("gkT", [128, 8, TT], BF16)
        gk = self.dscr("gk", [128, TT // 128, 1024], BF16)
        gv = self.dscr("gv", [128, TT // 128, 2048], BF16)
        rbT = self.dscr("rbT", [128, 16, T], BF16)
        gaT = self.dscr("gaT", [128, 16, T], BF16)
        gbT = self.dscr("gbT", [128, 16, T], BF16)
        lrT = self.dscr("lrT", [16, 2, TT], F32)
        oaT = self.dscr("oaT", [128, 16, T], BF16)
        NCH = NBK + 2
        qtd = self.dscr("qtd", [128, NCH, 2, 8, 128], BF16)
        atd = self.dscr("atd", [128, NCH, 2, 512], BF16)
        khd = self.dscr("khd", [128, NCH, 2, 1024], BF16)
        ofT = self.dscr("ofT", [128, 16, T], F32)
        ogT = self.dscr("ogT", [128, 16, T], BF16)
        mT = self.dscr("mT", [128, 16, T], BF16)
        x1T = self.dscr("x1T", [128, NDC, T], F32)
        h2e = self.dscr("h2e", [128, NDC, T], BF16)
        gin1 = self.nc.dram_tensor("gin1", [2 * 8 * 128, 512], F32)
        gout1 = self.nc.dram_tensor("gout1", [8 * 2 * 8 * 128, 512], F32)
        gin2 = self.nc.dram_tensor("gin2", [128, 16], F32)
        gout2 = self.nc.dram_tensor("gout2", [8 * 128, 16], F32)
        gin3 = self.nc.dram_tensor("gin3", [128, 32], F32)
        gout3 = self.nc.dram_tensor("gout3", [8 * 128, 32], F32)
        RG = [list(range(8))]

        ps = [nc.alloc_psum_tensor("ps%d" % i, [128, 512], F32).ap() for i in range(8)]
        pb = [P.B("psum", i) for i in range(8)]
        self.pk = 0

        def bank():
            k = self.pk
            self.pk = (k + 1) % 8
            return ps[k], pb[k]

        for gi_n, (shard, gi, go) in enumerate(self.gath):
            gb_ = P.B("gath", gi_n)
            gb2_ = P.B("gatho", gi_n)
            self.st(gi.ap(), shard, [], [gb_])
            P.op("pool", lambda gi=gi, go=go: nc.gpsimd.collective_compute(
                "AllGather", ALU.bypass, replica_groups=[list(range(8))], ins=[gi.ap().opt()], outs=[go.ap().opt()]),
                [gb_], [gb2_], cc=True)
        if self.gather == "dry":
            wseed = self.din("wseed", [128, 16384])
            wd = self.wdummy.ap()
            nx = 128 * 16384
            for k_ in range(12):
                P.dma("sp", wd[k_ * nx:(k_ + 1) * nx].rearrange("(p n) -> p n", p=128), wseed, writes=[P.B("wd", k_)])
            P.barrier()
        if self.gath:
            P.barrier()
        B = P.B
        consts = self.sb([128, 9, 128], F32, "cst")
        c_b = B("cst")
        self.ld(consts, cst, [c_b])
        LinF, LinB, LstF, LstB = consts[:, 0, :], consts[:, 1, :], consts[:, 2, :], consts[:, 3, :]
        perm = consts[:, 6, :]
        small = self.sb([128, 54], F32, "small")
        sm_b = B("small")
        self.ld(small, smallc, [sm_b])
        gmix, gffn = small[:, 0:16], small[:, 16:32]
        qn, kn, glan, sinkr = small[:, 32:33], small[:, 33:34], small[:, 34:38], small[:, 38:54]
        selt = self.sb([128, 32], F32, "sel")
        sel_b = B("sel")
        self.ld(selt, sel, [sel_b])
        ones = self.sb([128, 128], F32, "ones")
        onesb = self.sb([128, 128], BF16, "onesb")
        on_b = B("ones")
        self.memset(ones, 1.0, [on_b])
        self.memset(onesb, 1.0, [on_b])
        self.epscol = self.sb([128, 1], F32, "eps")
        self.memset(self.epscol, EPS, [on_b])
        onesD = self.sb([128, 128], F32, "onesD")
        ones128 = self.sb([128, 128], F32, "ones128")
        ones512 = self.sb([128, 128], F32, "ones512")
        self.memset(onesD, 1.0 / D, [on_b])
        self.memset(ones128, 1.0 / 128, [on_b])
        self.memset(ones512, 1.0 / 512, [on_b])
        mk = self.sb([128, 4, 512], BF16, "masks")
        mk_b = B("masks")
        for g in range(4):
            P.dma("pool", mk[:, 0, g * 128:(g + 1) * 128], cst[:, 7, :], writes=[mk_b])
            P.dma("pool", mk[:, 1, g * 128:(g + 1) * 128], cst[:, 8, :], writes=[mk_b])
        P.dma("pool", mk[:, 2:4, :], pcm, writes=[mk_b])
        cm = self.sb([128, 2, 512], BF16, "cmask")
        for g in range(4):
            P.dma("pool", cm[:, 0, g * 128:(g + 1) * 128], cst[:, 4, :], writes=[mk_b])
            P.dma("pool", cm[:, 1, g * 128:(g + 1) * 128], cst[:, 5, :], writes=[mk_b])
        modall = self.sb([128, 96, 2], F32, "modall")
        mod_b = B("modall")
        A1, B1, A1c, B1c, A2 = (self.sb([128, 16], F32, "modv") for _ in range(5))
        g2c = self.sb([128, 16], F32, "g2c")
        g5c = self.sb([128, 16], F32, "g5c")
        modv_b = B("modv")
        Dcols = self.sb([128, NCH, 2, 8], F32, "Dcols")
        dc_b = B("Dcols")
        halb = self.sb([128, 2, 16], BF16, "halb")
        halb_b = B("halb")
        self.sb_base = self.sb_off

        sc = self.sb([128, NDC, 2], F32)
        scb = self.sb([128, NDC, 2], BF16)
        t0 = self.nb()
        self.ld(sc, cc, [t0])
        sg = self.sb([128, NDC, 2], F32)
        t1 = self.nb()
        self.act(sg, sc, AF.Sigmoid, [t0], [t1])
        t2 = self.nb()
        self.tt(scb, sc, sg, ALU.mult, [t0, t1], [t2])
        bm = self.sb([128, 96], F32)
        bm_b = self.nb()
        self.ld(bm, bmod, [bm_b])
        ws = self.WS(self, [wmod[g] for g in range(24)], [128, NDC, 512], 2, "wmod")
        pm, pmb = bank()
        for g in range(24):
            w, wb = ws.get(g)
            for j in range(4):
                ch = g * 4 + j
                for c in range(NDC):
                    self.mm(pm[:, 2 * ch:2 * ch + 2], w[:, c, j * 128:(j + 1) * 128], scb[:, c, :],
                            c == 0, c == NDC - 1, [wb, t2], [pmb])
        for i in range(2):
            self.tt(modall[:, :, i], pm[:, i:192:2], bm, ALU.add, [pmb, bm_b], [mod_b])
        mx = lambda i: modall[:, 16 * i:16 * (i + 1), 0]
        mc = lambda i: modall[:, 16 * i:16 * (i + 1), 1]
        self.stt(A1, mx(1), 1.0, gmix, ALU.add, ALU.mult, [mod_b, sm_b], [modv_b])
        self.cp(B1, mx(0), [mod_b], [modv_b])
        self.stt(A1c, mc(1), 1.0, gmix, ALU.add, ALU.mult, [mod_b, sm_b], [modv_b])
        self.cp(B1c, mc(0), [mod_b], [modv_b])
        self.stt(A2, mx(4), 1.0, gffn, ALU.add, ALU.mult, [mod_b, sm_b], [modv_b])
        B2 = mx(3)
        gate2 = mx(2)
        gate5 = mx(5)
        self.cp(g2c, gate2, [mod_b], [mod_b])
        self.cp(g5c, gate5, [mod_b], [mod_b])
        if self.stop_after <= 0:
            return self.finish([])

        self.phase_reset()
        hT = self.sb([128, NDC, TT], BF16, "hT")
        hT_b = [self.nb("hT") for _ in range(TT // 512)]
        base1 = self.sb_off
        xt = [self.sb([128, NDC, 512], F32, "xt") for _ in range(2)]
        xt_b = [self.nb() for _ in range(2)]
        sq = [self.sb([128, 512], F32, "sq") for _ in range(2)]
        sq_b = [self.nb() for _ in range(2)]
        rs = self.sb([128, 512], F32)
        rs_b = self.nb()
        rtmp = self.sb([128, 512], F32)
        rtmp_b = self.nb()
        hn = self.sb([128, 512], F32)
        hn_b = self.nb()
        for tt in range(TT // 512):
            x_, xb_ = xt[tt % 2], xt_b[tt % 2]
            self.ld(x_, xe[:, :, tt * 512:(tt + 1) * 512], [xb_])
            pss, pssb = bank()
            for c in range(NDC):
                self.act(sq[c % 2], x_[:, c, :], AF.Square, [xb_], [sq_b[c % 2]])
                self.mm(pss, onesD, sq[c % 2], c == 0, c == NDC - 1, [sq_b[c % 2], on_b], [pssb])
            self.rstd(rs, pss, [pssb], [rs_b], rtmp, rtmp_b)
            for c in range(NDC):
                self.tt(hn, x_[:, c, :], rs, ALU.mult, [xb_, rs_b], [hn_b])
                o = hT[:, c, tt * 512:(tt + 1) * 512]
                if tt < NT:
                    self.act(o, hn, AF.Identity, [hn_b, modv_b], [hT_b[tt]], bias=B1[:, c:c + 1], scale=A1[:, c:c + 1])
                else:
                    self.act(o[:, 0:256], hn[:, 0:256], AF.Identity, [hn_b, modv_b], [hT_b[tt]],
                             bias=B1[:, c:c + 1], scale=A1[:, c:c + 1])
                    self.act(o[:, 256:512], hn[:, 256:512], AF.Identity, [hn_b, modv_b], [hT_b[tt]],
                             bias=B1c[:, c:c + 1], scale=A1c[:, c:c + 1])
        P.barrier()
        self.sb_off = base1
        rc = self.sb([128, TT], F32, "ropeC")
        rsn = self.sb([128, TT], F32, "ropeS")
        rp_b = self.nb()
        self.ld(rc, ropeC, [rp_b])
        self.ld(rsn, ropeS, [rp_b])
        NST = 3
        stg = [self.sb([128, 512], BF16, "stg") for _ in range(NST)]
        stg_b = [self.nb() for _ in range(NST)]
        f1 = [self.sb([128, 512], F32, "f1") for _ in range(2)]
        f1_b = [self.nb() for _ in range(2)]
        f2 = [self.sb([128, 512], F32, "f2") for _ in range(2)]
        f2_b = [self.nb() for _ in range(2)]
        f3 = [self.sb([128, 512], F32, "f3") for _ in range(2)]
        f3_b = [self.nb() for _ in range(2)]
        self.k = 0
        glanb = self.sb([128, 4], F32)
        fm_jobs = [("q", qT, 16, NT), ("k", kT, 4, NT + 1), ("gq", gqT, 8, NT), ("gk", gkT, 8, NT + 1),
                   ("rb", rbT, 16, NT), ("ga", gaT, 16, NT), ("gb", gbT, 16, NT)]
        chunk_list = []
        for kind, dst, nch, ntl in fm_jobs:
            for ci in range(nch):
                chunk_list.append((kind, dst, ci, ntl))
        assert len(chunk_list) == 84
        wsf = self.WS(self, [wfm[g] for g in range(21)] + [wtm[g] for g in range(7)], [128, NDC, 512], 3, "wfm")
        for gi in range(21):
            w, wb = wsf.get(gi)
            for j in range(4):
                kind, dst, ci, ntl = chunk_list[gi * 4 + j]
                for tt in range(ntl):
                    pp, ppb = bank()
                    for c in range(NDC):
                        self.mm(pp, w[:, c, j * 128:(j + 1) * 128], hT[:, c, tt * 512:(tt + 1) * 512],
                                c == 0, c == NDC - 1, [wb, hT_b[tt]], [ppb])
                    k = self.k
                    self.k += 1
                    so, sob = stg[k % NST], stg_b[k % NST]
                    a1, a1b = f1[k % 2], f1_b[k % 2]
                    a2, a2b = f2[k % 2], f2_b[k % 2]
                    a3, a3b = f3[k % 2], f3_b[k % 2]
                    cols = slice(tt * 512, (tt + 1) * 512)
                    if kind in ("q", "k"):
                        gcol = qn if kind == "q" else kn
                        self.act(a1, pp, AF.Square, [ppb], [a1b])
                        p2, p2b = bank()
                        self.mm(p2, ones128, a1, True, True, [a1b, on_b], [p2b])
                        self.rstd(a2, p2, [p2b], [a2b], a1, a1b)
                        self.stt(a3, pp, gcol, a2, ALU.mult, ALU.mult, [ppb, a2b, sm_b], [a3b])
                        p3, p3b = bank()
                        self.mm(p3, perm, a3, True, True, [a3b, c_b], [p3b])
                        self.tt(a1, a3, rc[:, cols], ALU.mult, [a3b, rp_b], [a1b])
                        self.tt(a2, p3, rsn[:, cols], ALU.mult, [p3b, rp_b], [a2b])
                        self.tt(so, a1, a2, ALU.add, [a1b, a2b], [sob])
                    elif kind == "gq":
                        self.act(so, pp, AF.Copy, [ppb], [sob], scale=1.0 / 16.0)
                    elif kind == "gk":
                        self.act(so, pp, AF.Copy, [ppb], [sob])
                    elif kind == "rb":
                        self.act(a1, pp, AF.Silu, [ppb], [a1b])
                        self.ts(so, a1, glan[:, ci % 4:ci % 4 + 1], None, ALU.mult, None, [a1b, sm_b], [sob])
                    else:
                        self.act(so, pp, AF.Sigmoid, [ppb], [sob])
                    self.st(dst[:, ci, cols], so, [sob])
        wl = self.sb([128, NDC, 32], BF16)
        wl_b = self.nb()
        P.dma("pool", wl, wlr, writes=[wl_b])
        lro = [self.sb([16, 512], F32) for _ in range(2)]
        lro_b = [self.nb() for _ in range(2)]
        for dr in range(2):
            for tt in range(NT + 1):
                pp, ppb = bank()
                for c in range(NDC):
                    self.mm(pp[0:16, :], wl[:, c, dr * 16:(dr + 1) * 16], hT[:, c, tt * 512:(tt + 1) * 512],
                            c == 0, c == NDC - 1, [wl_b, hT_b[tt]], [ppb])
                self.act(lro[tt % 2], pp[0:16, :], AF.Copy, [ppb], [lro_b[tt % 2]])
                self.st(lrT[:, dr, tt * 512:(tt + 1) * 512], lro[tt % 2], [lro_b[tt % 2]])
        tm_jobs = [(vA, 0)] + [(gk, i) for i in range(2)] + [(gv, i) for i in range(4)]
        for gi in range(7):
            w, wb = wsf.get(21 + gi)
            dst, gi2 = tm_jobs[gi]
            for bl in range(TT // 128):
                if dst is not vA and (bl >= NBK and bl < NBK + 2):
                    continue
                pp, ppb = bank()
                tt = bl // 4
                for c in range(NDC):
                    self.mm(pp, hT[:, c, bl * 128:(bl + 1) * 128], w[:, c, :], c == 0, c == NDC - 1,
                            [wb, hT_b[tt]], [ppb])
                k = self.k
                self.k += 1
                so, sob = stg[k % NST], stg_b[k % NST]
                self.act(so, pp, AF.Copy, [ppb], [sob])
                self.st(dst[:, bl, gi2 * 512:(gi2 + 1) * 512], so, [sob])
        if self.stop_after <= 2:
            return self.finish([])

        self.phase_reset()
        qs = self.sb([128, 16, T], BF16, "qs")
        ks = self.sb([128, 4, TT], BF16, "ks")
        vs = self.sb([128, TT // 128, 512], BF16, "vs")
        ost = [self.sb([128, 4, 128], BF16, "ost") for _ in range(3)]
        ost_b = [self.nb() for _ in range(3)]
        in_b = self.nb()
        for hh in range(4):
            self.ld(qs[:, 4 * hh:4 * hh + 4, :], qT[:, 4 * hh:4 * hh + 4, :], [in_b])
        self.ld(ks, kT, [in_b])
        self.ld(vs, vA, [in_b])
        sx = self.sb([128, 16], F32)
        sx_b = self.nb()
        self.act(sx, sinkr, AF.Exp, [sm_b], [sx_b])
        SE = self.sb([128, 16, 128], F32, "SE")
        se_b = self.nb()
        self.cp(SE, sx.unsqueeze(2).to_broadcast([128, 16, 128]), [sx_b], [se_b])
        pT = [self.sb([128, 5, 512], BF16, "pT") for _ in range(2)]
        pT_b = [[self.nb() for _ in range(5)] for _ in range(2)]
        den = [self.sb([128, 512], F32, "den") for _ in range(2)]
        den_b = [self.nb() for _ in range(2)]
        oa_b = self.nb()
        it = 0
        SCL = 128.0 ** -0.5
        for i in range(NBK):
            for hh in range(4):
                cur = it % 2
                it += 1
                p_, p_b = pT[cur], pT_b[cur]
                kbl = [(i - 1) * 128 if i > 0 else HL, i * 128, (i + 1) * 128 if i < NBK - 1 else HR, CX, CX + 128]
                mski = [2 if i == 0 else 0, None, 3 if i == NBK - 1 else 1, None, None]
                qap = qs[:, 4 * hh:4 * hh + 4, i * 128:(i + 1) * 128]
                for kb_ in range(5):
                    pp, ppb = bank()
                    self.mm(pp.rearrange("p (g q) -> p g q", g=4), ks[:, hh, kbl[kb_]:kbl[kb_] + 128], qap, True, True, [in_b], [ppb])
                    self.act(p_[:, kb_, :], pp, AF.Exp, [ppb], [p_b[kb_]], scale=SCL)
                    if mski[kb_] is not None:
                        self.tt(p_[:, kb_, :], p_[:, kb_, :], mk[:, mski[kb_], :], ALU.mult, [p_b[kb_], mk_b], [p_b[kb_]], eng="pool")
                po, pob = bank()
                for kb_ in range(5):
                    vb = kbl[kb_] // 128
                    self.mm(po, vs[:, vb, hh * 128:(hh + 1) * 128], p_[:, kb_, :], kb_ == 0, kb_ == 4, [in_b, p_b[kb_]], [pob])
                pd, pdb = bank()
                for kb_ in range(5):
                    self.mm(pd, onesb, p_[:, kb_, :], kb_ == 0, kb_ == 4, [on_b, p_b[kb_]], [pdb])
                dn, dnb = den[cur], den_b[cur]
                self.tt(dn, pd, SE[:, 4 * hh:4 * hh + 4, :].rearrange("p g q -> p (g q)"), ALU.add, [pdb, se_b], [dnb])
                self.recip(dn, dn, [dnb], [dnb])
                os_, osb = ost[it % 3], ost_b[it % 3]
                self.tt(os_, po.rearrange("p (g q) -> p g q", g=4),
                        dn.rearrange("p (g q) -> p g q", g=4), ALU.mult, [pob, dnb], [osb])
                self.st(oaT[:, 4 * hh:4 * hh + 4, i * 128:(i + 1) * 128], os_, [osb])
        if self.stop_after <= 3:
            return self.finish([])

        self.phase_reset()
        cblk = lambda c: c if c < NBK else (CX // 128 + (c - NBK))
        wgs = self.sb([16, 2, 1024], F32)
        bgs = self.sb([1, 2, 1024], F32)
        wg_b = self.nb()
        self.ld(wgs, wg.rearrange("d k n -> k d n"), [wg_b])
        self.ld(bgs, bg.rearrange("d k n -> k d n"), [wg_b])
        lrs = self.sb([16, 2, TT], F32)
        self.ld(lrs, lrT, [wg_b])
        spt = self.sb([128, 2, 1024], F32, "sp")
        spt_b = [self.nb(), self.nb()]
        ex = [self.sb([128, 1024], F32, "ex") for _ in range(2)]
        ex_b = [self.nb() for _ in range(2)]
        gkt = [self.sb([128, 1024], BF16, "gkt") for _ in range(2)]
        gkt_b = [self.nb() for _ in range(2)]
        gqs = [self.sb([128, 8, 128], BF16, "gqs") for _ in range(2)]
        gks = [self.sb([128, 8, 128], BF16, "gks") for _ in range(2)]
        gin_b = [self.nb() for _ in range(2)]
        E1 = [self.sb([128, 8, 128], F32, "E1") for _ in range(2)]
        E1_b = [self.nb() for _ in range(2)]
        E2 = [self.sb([128, 8, 128], F32, "E2") for _ in range(2)]
        E2_b = [self.nb() for _ in range(2)]
        qt_s = [self.sb([128, 8, 128], BF16, "qts") for _ in range(2)]
        qt_b = [self.nb() for _ in range(2)]
        kt_s = [self.sb([128, 8, 128], BF16, "kts") for _ in range(2)]
        kt_b = [self.nb() for _ in range(2)]
        kh_s = [self.sb([128, 1024], BF16, "khs") for _ in range(2)]
        kh_b = [self.nb() for _ in range(2)]
        at_s = [self.sb([128, 512], BF16, "ats") for _ in range(2)]
        at_b = [self.nb() for _ in range(2)]
        Lin = [LinF, LinB]
        Lst = [LstF, LstB]
        for c in range(NCH):
            blk = cblk(c)
            tcol = slice(blk * 128, (blk + 1) * 128)
            own = c < NBK
            gi_ = c % 2
            self.ld(gkt[gi_], gk[:, blk, :], [gkt_b[gi_]])
            if own:
                self.ld(gqs[gi_], gqT[:, :, tcol], [gin_b[gi_]])
                self.ld(gks[gi_], gkT[:, :, tcol], [gin_b[gi_]])
            for dr in range(2):
                k = dr
                for half in range(2):
                    pp, ppb = bank()
                    self.mm(pp, lrs[:, dr, tcol], wgs[:, dr, half * 512:(half + 1) * 512], True, False, [wg_b], [ppb])
                    self.mm(pp, ones[0:1, :], bgs[:, dr, half * 512:(half + 1) * 512], False, True, [wg_b, on_b], [ppb])
                    self.act(ex[k][:, half * 512:(half + 1) * 512], pp, AF.Exp, [ppb], [ex_b[k]], scale=-1.0)
                self.act(spt[:, dr, :], ex[k], AF.Ln, [ex_b[k]], [spt_b[dr]], bias=1.0)
                for half in range(2):
                    pp, ppb = bank()
                    self.mm(pp, Lst[dr], spt[:, dr, half * 512:(half + 1) * 512], True, True, [c_b, spt_b[dr]], [ppb])
                    self.act(ex[k][:, half * 512:(half + 1) * 512], pp, AF.Exp, [ppb], [ex_b[k]])
                self.tt(kh_s[dr], gkt[gi_], ex[k], ALU.mult, [gkt_b[gi_], ex_b[k]], [kh_b[dr]])
                self.st(khd[:, c, dr, :], kh_s[dr], [kh_b[dr]])
                for half in range(2):
                    pp, ppb = bank()
                    for q4 in range(4):
                        i8 = half * 4 + q4
                        self.mm(pp[:, q4 * 128:(q4 + 1) * 128], spt[:, dr, i8 * 128:(i8 + 1) * 128], Lin[dr], True, True,
                                [spt_b[dr], c_b], [ppb])
                    self.act(E1[dr][:, half * 4:half * 4 + 4, :].rearrange("p a t -> p (a t)"), pp, AF.Exp, [ppb], [E1_b[dr]])
                    if own:
                        self.act(E2[dr][:, half * 4:half * 4 + 4, :].rearrange("p a t -> p (a t)"), pp, AF.Exp, [ppb], [E2_b[dr]], scale=-1.0)
                last = 127 if dr == 0 else 0
                self.cp(Dcols[:, c, dr, :], E1[dr][:, :, last], [E1_b[dr]], [dc_b])
                if own:
                    self.tt(qt_s[dr], gqs[gi_], E1[dr], ALU.mult, [gin_b[gi_], E1_b[dr]], [qt_b[dr]])
                    self.tt(kt_s[dr], gks[gi_], E2[dr], ALU.mult, [gin_b[gi_], E2_b[dr]], [kt_b[dr]])
                    self.st(qtd[:, c, dr, :, :], qt_s[dr], [qt_b[dr]])
                    pp, ppb = bank()
                    for hh in range(4):
                        for hf in range(2):
                            self.mm(pp[:, hh * 128:(hh + 1) * 128], kt_s[dr][:, 2 * hh + hf, :], qt_s[dr][:, 2 * hh + hf, :],
                                    hf == 0, hf == 1, [kt_b[dr], qt_b[dr]], [ppb])
                    self.tt(at_s[dr], pp, cm[:, dr, :], ALU.mult, [ppb, mk_b], [at_b[dr]])
                    self.st(atd[:, c, dr, :], at_s[dr], [at_b[dr]])
        if self.stop_after <= 4:
            return self.finish([])

        self.phase_reset()
        Sin = [self.sb([128, 8, 512], F32, "Sin") for _ in range(2)]
        Sin_b = [self.nb() for _ in range(2)]
        base4 = self.sb_off
        Sf = self.sb([128, 8, 512], F32, "Sf")
        S_b = self.nb()
        Sctx = [self.sb([128, 8, 512], F32, "Sctx") for _ in range(2)]
        Sctx_b = [self.nb() for _ in range(2)]
        Dt = self.sb([128, 2, 8], F32, "Dt")
        Dt_b = self.nb()
        khl = [self.sb([128, 1024], BF16, "khl") for _ in range(2)]
        khl_b = [self.nb() for _ in range(2)]
        gvl = [self.sb([128, 2048], BF16, "gvl") for _ in range(2)]
        gvl_b = [self.nb() for _ in range(2)]
        self.li = 0

        def state_step(c, dr, first):
            li = self.li % 2
            self.li += 1
            self.ld(khl[li], khd[:, c, dr, :], [khl_b[li]])
            self.ld(gvl[li], gv[:, cblk(c), :], [gvl_b[li]])
            for i8 in range(8):
                hh = i8 // 2
                pp, ppb = bank()
                self.mm(pp, khl[li][:, i8 * 128:(i8 + 1) * 128], gvl[li][:, hh * 512:(hh + 1) * 512], True, True,
                        [khl_b[li], gvl_b[li]], [ppb])
                if first:
                    self.cp(Sf[:, i8, :], pp, [ppb], [S_b])
                else:
                    self.stt(Sf[:, i8, :], Sf[:, i8, :], Dcols[:, c, dr, i8:i8 + 1], pp, ALU.mult, ALU.add,
                             [S_b, ppb, dc_b], [S_b])

        for dr in range(2):
            order = [NBK, NBK + 1] if dr == 0 else [NBK + 1, NBK]
            for n, c in enumerate(order):
                state_step(c, dr, n == 0)
            self.cp(Sctx[dr], Sf, [S_b], [Sctx_b[dr]], eng="pool")
            order = list(range(NBK)) if dr == 0 else list(range(NBK - 1, -1, -1))
            for n, c in enumerate(order):
                state_step(c, dr, n == 0)
                if n == 0:
                    self.cp(Dt[:, dr, :], Dcols[:, c, dr, :], [dc_b], [Dt_b])
                else:
                    self.tt(Dt[:, dr, :], Dt[:, dr, :], Dcols[:, c, dr, :], ALU.mult, [dc_b, Dt_b], [Dt_b])
            g1 = P.B("gin1")
            self.st(gin1.ap()[dr * 1024:(dr + 1) * 1024, :].rearrange("(a p) n -> p a n", p=128), Sf, [S_b], [g1])
        g2 = P.B("gin2")
        self.st(gin2.ap(), Dt.rearrange("p a b -> p (a b)"), [Dt_b], [g2])
        go1, go2 = P.B("gout1"), P.B("gout2")
        if self.stop_after <= 4.2:
            return self.finish([])
        P.op("pool", lambda: nc.gpsimd.collective_compute("AllGather", ALU.bypass, replica_groups=RG,
                                                          ins=[gin1.ap().opt()], outs=[gout1.ap().opt()]), [g1], [go1], cc=1)
        P.op("pool", lambda: nc.gpsimd.collective_compute("AllGather", ALU.bypass, replica_groups=RG,
                                                          ins=[gin2.ap().opt()], outs=[gout2.ap().opt()]), [g2, go1], [go2], cc=2)
        Dg = self.sb([128, 8, 16], F32, "Dg")
        Dg_b = self.nb()
        self.ld(Dg, gout2.ap().rearrange("(r p) n -> p r n", p=128), [Dg_b], reads=[go2])
        Dp = self.sb([128, 8, 2, 8], F32, "Dp")
        Dp_b = self.nb()
        for r in range(8):
            for dr in range(2):
                mcol = selt[:, dr * 8 + r:dr * 8 + r + 1]
                self.ts(Dp[:, r, dr, :], Dg[:, r, dr * 8:(dr + 1) * 8], -1.0, mcol, ALU.add, ALU.mult, [Dg_b, sel_b], [Dp_b])
        self.ts(Dp, Dp, 1.0, None, ALU.add, None, [Dp_b], [Dp_b])
        Sl = [self.sb([128, 8, 512], F32, "Sl") for _ in range(2)]
        Sl_b = [self.nb() for _ in range(2)]
        go1v = gout1.ap().rearrange("(r d a p) n -> r d p a n", r=8, d=2, a=8, p=128)
        n = 0
        for dr in range(2):
            self.cp(Sin[dr], Sctx[dr], [Sctx_b[dr]], [Sin_b[dr]], eng="pool")
            order = range(8) if dr == 0 else range(7, -1, -1)
            for r in order:
                sl, slb = Sl[n % 2], Sl_b[n % 2]
                n += 1
                self.ld(sl, go1v[r, dr], [slb], reads=[go1])
                mcol = selt[:, dr * 8 + r:dr * 8 + r + 1]
                self.ts(sl, sl, mcol, None, ALU.mult, None, [slb, sel_b], [slb], eng="pool")
                for i8 in range(8):
                    self.stt(Sin[dr][:, i8, :], Sin[dr][:, i8, :], Dp[:, r, dr, i8:i8 + 1], sl[:, i8, :], ALU.mult, ALU.add,
                             [Sin_b[dr], slb, Dp_b], [Sin_b[dr]])

        if self.stop_after <= 4.4:
            return self.finish([])
        P.barrier()
        self.sb_off = base4
        khl = [self.sb([128, 1024], BF16, "khl") for _ in range(2)]
        khl_b = [self.nb() for _ in range(2)]
        gvl = [self.sb([128, 2048], BF16, "gvl") for _ in range(2)]
        gvl_b = [self.nb() for _ in range(2)]
        Sb16 = self.sb([128, 8, 512], BF16, "Sb16")
        Sb16_b = self.nb()
        qtl = [self.sb([128, 8, 128], BF16, "qtl") for _ in range(2)]
        atl = [self.sb([128, 512], BF16, "atl") for _ in range(2)]
        oin_b = [self.nb() for _ in range(2)]
        ofs = [self.sb([128, 16, 128], F32, "ofs") for _ in range(2)]
        ofs_b = [self.nb() for _ in range(2)]
        ofl = [self.sb([128, 16, 128], F32, "ofl") for _ in range(2)]
        ofl_b = [self.nb() for _ in range(2)]
        osq = self.sb([128, 16, 128], F32, "osq")
        osq_b = self.nb()
        rst = self.sb([128, 4, 128], F32, "rst")
        rst_b = self.nb()
        rst2 = self.sb([128, 4, 128], F32, "rst2")
        rst2_b = self.nb()
        rbl = [self.sb([128, 16, 128], BF16, "rbl") for _ in range(2)]
        rbl_b = [self.nb() for _ in range(2)]
        ogs = [self.sb([128, 16, 128], BF16, "ogs") for _ in range(2)]
        ogs_b = [self.nb() for _ in range(2)]
        for dr in range(2):
            if dr == 1 and self.stop_after <= 4.6:
                return self.finish([])
            S = Sin[dr]
            Sb_ = Sin_b[dr]
            order = list(range(NBK)) if dr == 0 else list(range(NBK - 1, -1, -1))
            for n, c in enumerate(order):
                li = self.li % 2
                self.li += 1
                tcol = slice(c * 128, (c + 1) * 128)
                self.ld(khl[li], khd[:, c, dr, :], [khl_b[li]])
                self.ld(gvl[li], gv[:, c, :], [gvl_b[li]])
                self.ld(qtl[li], qtd[:, c, dr, :, :], [oin_b[li]])
                self.ld(atl[li], atd[:, c, dr, :], [oin_b[li]])
                if dr == 1:
                    self.ld(ofl[li], ofT[:, :, tcol], [ofl_b[li]], reads=[P.B("ofT", c)])
                    self.ld(rbl[li], rbT[:, :, tcol], [rbl_b[li]])
                self.act(Sb16.rearrange("p a n -> p (a n)"), S.rearrange("p a n -> p (a n)"), AF.Copy, [Sb_], [Sb16_b])
                o_, o_b = ofs[li], ofs_b[li]
                for hh in range(4):
                    pp, ppb = bank()
                    for dvc in range(4):
                        oc = pp[:, dvc * 128:(dvc + 1) * 128]
                        for hf in range(2):
                            self.mm(oc, Sb16[:, 2 * hh + hf, dvc * 128:(dvc + 1) * 128], qtl[li][:, 2 * hh + hf, :],
                                    hf == 0, False, [Sb16_b, oin_b[li]], [ppb])
                        self.mm(oc, gvl[li][:, hh * 512 + dvc * 128:hh * 512 + (dvc + 1) * 128], atl[li][:, hh * 128:(hh + 1) * 128],
                                False, True, [gvl_b[li], oin_b[li]], [ppb])
                    dst = o_[:, 4 * hh:4 * hh + 4, :].rearrange("p a t -> p (a t)")
                    self.act(dst, pp, AF.Copy, [ppb], [o_b])
                if dr == 1:
                    self.tt(o_, o_, ofl[li], ALU.add, [o_b, ofl_b[li]], [o_b], eng="pool")
                if dr == 0:
                    self.st(ofT[:, :, tcol], o_, [o_b], [P.B("ofT", c)])
                else:
                    import os as _os
                    _skip = _os.environ.get("GLA_SKIP", "")
                    if "sq" not in _skip:
                        self.act(osq.rearrange("p a t -> p (a t)"), o_.rearrange("p a t -> p (a t)"), AF.Square, [o_b], [osq_b])
                    pp, ppb = bank()
                    if "mm" not in _skip:
                        for hh in range(4):
                            for dvc in range(4):
                                self.mm(pp[:, hh * 128:(hh + 1) * 128], ones512, osq[:, 4 * hh + dvc, :], dvc == 0, dvc == 3,
                                        [osq_b, on_b], [ppb])
                    if "rs" not in _skip:
                        self.rstd(rst.rearrange("p a t -> p (a t)"), pp, [ppb], [rst_b], rst2.rearrange("p a t -> p (a t)"), rst2_b)
                    if "bc" not in _skip:
                        for hh in range(4):
                            self.tt(o_[:, 4 * hh:4 * hh + 4, :], o_[:, 4 * hh:4 * hh + 4, :],
                                    rst[:, hh:hh + 1, :].to_broadcast([128, 4, 128]), ALU.mult, [o_b, rst_b], [o_b])
                    self.tt(ogs[li], o_, rbl[li], ALU.mult, [o_b, rbl_b[li]], [ogs_b[li]])
                    self.st(ogT[:, :, tcol], ogs[li], [ogs_b[li]])
                for i8 in range(8):
                    hh = i8 // 2
                    pp, ppb = bank()
                    self.mm(pp, khl[li][:, i8 * 128:(i8 + 1) * 128], gvl[li][:, hh * 512:(hh + 1) * 512], True, True,
                            [khl_b[li], gvl_b[li]], [ppb])
                    self.stt(S[:, i8, :], S[:, i8, :], Dcols[:, c, dr, i8:i8 + 1], pp, ALU.mult, ALU.add,
                             [Sb_, ppb, dc_b], [Sb_])
        if self.stop_after <= 5:
            return self.finish([])

        self.phase_reset()
        oas = self.sb([128, 16, T], BF16, "oas")
        ogs2 = self.sb([128, 16, T], BF16, "ogs2")
        in_b = self.nb()
        for hh in range(4):
            self.ld(oas[:, 4 * hh:4 * hh + 4, :], oaT[:, 4 * hh:4 * hh + 4, :], [in_b])
            self.ld(ogs2[:, 4 * hh:4 * hh + 4, :], ogT[:, 4 * hh:4 * hh + 4, :], [in_b])
        wsa = self.WS(self, [wao[g] for g in range(8)], [128, NDC, 256], 2, "wao")
        wsg = self.WS(self, [wgo[g] for g in range(8)], [128, NDC, 256], 2, "wgo")
        gal = [self.sb([128, 512], BF16, "gal") for _ in range(2)]
        gbl = [self.sb([128, 512], BF16, "gbl") for _ in range(2)]
        gl_b = [self.nb() for _ in range(2)]
        m1 = [self.sb([128, 512], F32, "m1") for _ in range(2)]
        m1_b = [self.nb() for _ in range(2)]
        m2 = [self.sb([128, 512], F32, "m2") for _ in range(2)]
        m2_b = [self.nb() for _ in range(2)]
        mo = [self.sb([128, 512], BF16, "mo") for _ in range(2)]
        mo_b = [self.nb() for _ in range(2)]
        k = 0
        for g in range(8):
            wa, wab = wsa.get(g)
            wg_, wgb = wsg.get(g)
            for j in range(2):
                ci = g * 2 + j
                for tt in range(NT):
                    cols = slice(tt * 512, (tt + 1) * 512)
                    kk = k % 2
                    k += 1
                    self.ld(gal[kk], gaT[:, ci, cols], [gl_b[kk]])
                    self.ld(gbl[kk], gbT[:, ci, cols], [gl_b[kk]])
                    pa, pab = bank()
                    for c in range(NDC):
                        self.mm(pa, wa[:, c, j * 128:(j + 1) * 128], oas[:, c, cols], c == 0, c == NDC - 1, [wab, in_b], [pab])
                    pg, pgb = bank()
                    for c in range(NDC):
                        self.mm(pg, wg_[:, c, j * 128:(j + 1) * 128], ogs2[:, c, cols], c == 0, c == NDC - 1, [wgb, in_b], [pgb])
                    self.act(m1[kk], pa, AF.Copy, [pab], [m1_b[kk]])
                    self.act(m2[kk], pg, AF.Copy, [pgb], [m2_b[kk]])
                    self.tt(m1[kk], m1[kk], gal[kk], ALU.mult, [m1_b[kk], gl_b[kk]], [m1_b[kk]])
                    self.tt(m2[kk], m2[kk], gbl[kk], ALU.mult, [m2_b[kk], gl_b[kk]], [m2_b[kk]], eng="pool")
                    self.tt(mo[kk], m1[kk], m2[kk], ALU.add, [m1_b[kk], m2_b[kk]], [mo_b[kk]], eng="pool")
                    self.st(mT[:, ci, cols], mo[kk], [mo_b[kk]])
        self.phase_reset()
        ms = self.sb([128, 16, T], BF16, "ms")
        in_b = self.nb()
        for hh in range(4):
            self.ld(ms[:, 4 * hh:4 * hh + 4, :], mT[:, 4 * hh:4 * hh + 4, :], [in_b])
        wso = self.WS(self, [wo[g] for g in range(8)], [128, NDC, 256], 2, "wo")
        xl = [self.sb([128, 512], F32, "xl") for _ in range(3)]
        xl_b = [self.nb() for _ in range(3)]
        xo = [self.sb([128, 512], F32, "xo") for _ in range(3)]
        xo_b = [self.nb() for _ in range(3)]
        k = 0
        for g in range(8):
            w, wb = wso.get(g)
            for j in range(2):
                ci = g * 2 + j
                for tt in range(NT):
                    cols = slice(tt * 512, (tt + 1) * 512)
                    kk = k % 3
                    k += 1
                    self.ld(xl[kk], xe[:, ci, cols], [xl_b[kk]])
                    pp, ppb = bank()
                    for c in range(NDC):
                        self.mm(pp, w[:, c, j * 128:(j + 1) * 128], ms[:, c, cols], c == 0, c == NDC - 1, [wb, in_b], [ppb])
                    self.act(xo[kk], pp, AF.Identity, [ppb, mod_b], [xo_b[kk]], scale=g2c[:, ci:ci + 1])
                    self.tt(xo[kk], xo[kk], xl[kk], ALU.add, [xo_b[kk], xl_b[kk]], [xo_b[kk]], eng="pool")
                    self.st(x1T[:, ci, cols], xo[kk], [xo_b[kk]])
        if self.stop_after <= 6:
            return self.finish([])

        self.phase_reset()
        xt = [self.sb([128, NDC, 512], F32, "xt") for _ in range(2)]
        xt_b = [self.nb() for _ in range(2)]
        sq = [self.sb([128, 512], F32, "sq") for _ in range(2)]
        sq_b = [self.nb() for _ in range(2)]
        rs = self.sb([128, 512], F32)
        rs_b = self.nb()
        rtmp = self.sb([128, 512], F32)
        rtmp_b = self.nb()
        hn = self.sb([128, 512], F32)
        hn_b = self.nb()
        hst = [self.sb([128, NDC, 512], BF16, "hst") for _ in range(2)]
        hst_b = [self.nb() for _ in range(2)]
        edge = self.sb([128, 2, 16], F32, "edge")
        edge_b = self.nb()
        h2b = P.B("h2e")
        for tt in range(NT):
            x_, xb_ = xt[tt % 2], xt_b[tt % 2]
            hs, hsb = hst[tt % 2], hst_b[tt % 2]
            self.ld(x_, x1T[:, :, tt * 512:(tt + 1) * 512], [xb_])
            pss, pssb = bank()
            for c in range(NDC):
                self.act(sq[c % 2], x_[:, c, :], AF.Square, [xb_], [sq_b[c % 2]])
                self.mm(pss, onesD, sq[c % 2], c == 0, c == NDC - 1, [sq_b[c % 2], on_b], [pssb])
            self.rstd(rs, pss, [pssb], [rs_b], rtmp, rtmp_b)
            for c in range(NDC):
                self.tt(hn, x_[:, c, :], rs, ALU.mult, [xb_, rs_b], [hn_b])
                self.act(hs[:, c, :], hn, AF.Identity, [hn_b, modv_b, mod_b], [hsb], bias=B2[:, c:c + 1], scale=A2[:, c:c + 1])
            self.st(h2e[:, :, tt * 512:(tt + 1) * 512], hs, [hsb], [h2b])
            if tt == 0:
                self.cp(edge[:, 0, :], hs[:, :, 0], [hsb], [edge_b])
            if tt == NT - 1:
                self.cp(edge[:, 1, :], hs[:, :, 511], [hsb], [edge_b])
        g3, go3 = P.B("gin3"), P.B("gout3")
        self.st(gin3.ap(), edge.rearrange("p a b -> p (a b)"), [edge_b], [g3])
        P.op("pool", lambda: nc.gpsimd.collective_compute("AllGather", ALU.bypass, replica_groups=RG,
                                                          ins=[gin3.ap().opt()], outs=[gout3.ap().opt()]), [g3], [go3], cc=3)
        Gh = self.sb([128, 8, 2, 16], F32, "Gh")
        Gh_b = self.nb()
        self.ld(Gh, gout3.ap().rearrange("(r p) (a n) -> p r a n", p=128, a=2), [Gh_b], reads=[go3])
        hal = self.sb([128, 2, 16], F32, "hal")
        hal_b = self.nb()
        for side in range(2):
            a = 1 - side
            for r in range(8):
                scol = selt[:, 16 + side * 8 + r:16 + side * 8 + r + 1]
                if r == 0:
                    self.ts(hal[:, side, :], Gh[:, r, a, :], scol, None, ALU.mult, None, [Gh_b, sel_b], [hal_b])
                else:
                    self.stt(hal[:, side, :], Gh[:, r, a, :], scol, hal[:, side, :], ALU.mult, ALU.add, [Gh_b, sel_b, hal_b], [hal_b])
        self.cp(halb, hal, [hal_b], [halb_b])

        self.phase_reset()
        cw = self.sb([128, 88, 3], F32, "cw")
        cbs = self.sb([128, 88], F32, "cb")
        cw_b = self.nb()
        self.ld(cw, convw, [cw_b])
        self.ld(cbs, convb, [cw_b])
        base6 = self.sb_off
        TH = T // 2
        NP = 3
        PW = (TH + 2) // NP
        assert PW * NP == TH + 2 and PW <= 512
        fin = []
        for hf in range(2):
            self.sb_off = base6
            if hf:
                P.barrier()
            actT = self.sb([128, NFC, TH], BF16, "actT")
            act_b = self.nb()
            keep = self.sb_off
            h2s = self.sb([128, NDC, TH + 2], BF16, "h2s")
            h2s_b = self.nb()
            if hf == 0:
                self.ld(h2s[:, :, 1:TH + 2], h2e[:, :, 0:TH + 1], [h2s_b], reads=[h2b])
                self.cp(h2s[:, :, 0], halb[:, 0, :], [halb_b], [h2s_b])
            else:
                self.ld(h2s[:, :, 0:TH + 1], h2e[:, :, TH - 1:T], [h2s_b], reads=[h2b])
                self.cp(h2s[:, :, TH + 1], halb[:, 1, :], [halb_b], [h2s_b])
            wua = self.WS(self, [wup[0, g] for g in range(22)], [128, NDC, 256], 2, "wua")
            wug = self.WS(self, [wup[1, g] for g in range(22)], [128, NDC, 256], 2, "wug")
            U = [[self.sb([128, TH + 2], F32, "U") for _ in range(1)] for _ in range(2)]
            U_b = [[self.nb() for _ in range(1)] for _ in range(2)]
            acc = [[self.sb([128, TH], F32, "acc") for _ in range(1)] for _ in range(2)]
            acc_b = [[self.nb() for _ in range(1)] for _ in range(2)]
            k = 0
            for g in range(22):
                wa, wab = wua.get(g)
                wg_, wgb = wug.get(g)
                for j in range(2):
                    jc = g * 2 + j
                    kk = 0
                    k += 1
                    for br, (w, wb) in enumerate(((wa, wab), (wg_, wgb))):
                        u, ub = U[br][kk], U_b[br][kk]
                        for pi in range(NP):
                            pp, ppb = bank()
                            for c in range(NDC):
                                self.mm(pp[:, 0:PW], w[:, c, j * 128:(j + 1) * 128], h2s[:, c, pi * PW:(pi + 1) * PW],
                                        c == 0, c == NDC - 1, [wb, h2s_b], [ppb])
                            self.act(u[:, pi * PW:(pi + 1) * PW], pp[:, 0:PW], AF.Copy, [ppb], [ub])
                        ch = br * NFC + jc
                        a_, ab_ = acc[br][kk], acc_b[br][kk]
                        self.ts(a_, u[:, 1:TH + 1], cw[:, ch, 1:2], cbs[:, ch:ch + 1], ALU.mult, ALU.add, [ub, cw_b], [ab_])
                        self.stt(a_, u[:, 0:TH], cw[:, ch, 0:1], a_, ALU.mult, ALU.add, [ub, cw_b, ab_], [ab_])
                        self.stt(a_, u[:, 2:TH + 2], cw[:, ch, 2:3], a_, ALU.mult, ALU.add, [ub, cw_b, ab_], [ab_])
                    self.act(acc[0][kk], acc[0][kk], AF.Silu, [acc_b[0][kk]], [acc_b[0][kk]])
                    self.tt(actT[:, jc, :], acc[0][kk], acc[1][kk], ALU.mult, [acc_b[0][kk], acc_b[1][kk]], [act_b], eng="pool")
            P.barrier()
            self.sb_off = keep
            wsd = self.WS(self, [wdn[g] for g in range(8)], [128, NFC, 256], 2, "wdn")
            xl = [self.sb([128, 512], F32, "xl") for _ in range(3)]
            xl_b = [self.nb() for _ in range(3)]
            xo = [self.sb([128, 512], F32, "xo") for _ in range(3)]
            xo_b = [self.nb() for _ in range(3)]
            k = 0
            for g in range(8):
                w, wb = wsd.get(g)
                for j in range(2):
                    ci = g * 2 + j
                    for t2 in range(TH // 512):
                        cols = slice(hf * TH + t2 * 512, hf * TH + (t2 + 1) * 512)
                        kk = k % 3
                        k += 1
                        self.ld(xl[kk], x1T[:, ci, cols], [xl_b[kk]])
                        pp, ppb = bank()
                        for c in range(NFC):
                            self.mm(pp, w[:, c, j * 128:(j + 1) * 128], actT[:, c, t2 * 512:(t2 + 1) * 512],
                                    c == 0, c == NFC - 1, [wb, act_b], [ppb])
                        self.act(xo[kk], pp, AF.Identity, [ppb, mod_b], [xo_b[kk]], scale=g5c[:, ci:ci + 1])
                        self.tt(xo[kk], xo[kk], xl[kk], ALU.add, [xo_b[kk], xl_b[kk]], [xo_b[kk]], eng="pool")
                        fin.append(self.st(outT[:, ci, cols], xo[kk], [xo_b[kk]]))
        return self.finish(fin)

    def finish(self, fin):
        self.P.barrier()
        self.P.emit(final_wait_ops=fin)
        return self.nc


def _fm(W, ncol):
    K, N = W.shape
    return np.ascontiguousarray(W.reshape(K // 128, 128, N // ncol, ncol).transpose(2, 1, 0, 3))


def _cols(v):
    return np.ascontiguousarray(v.reshape(-1, 128).T)


def host_prep(inp, T, gather=False):
    f32 = np.float32
    SEQ = 4 * T
    x, c, ctx, c_ctx = inp["x"], inp["c"], inp["ctx"], inp["c_ctx"]
    w_in = inp["w_in"][0]
    sh = {}
    sh["wmod"] = _fm(inp["w_mod"][0], 512)
    sh["bmod"] = _cols(inp["b_mod"][0])
    o = np.cumsum([0, 2048, 512, 512, 1024, 1024, 2048, 2048, 16, 16, 2048, 2048])
    seg = lambda i: w_in[:, o[i]:o[i + 1]]
    sh["wfm"] = _fm(np.concatenate([seg(0), seg(1), seg(3), seg(4), seg(6), seg(9), seg(10)], axis=1), 512)
    sh["wlr"] = _fm(np.concatenate([seg(7), seg(8)], axis=1), 32)[0]
    sh["wtm"] = _fm(np.concatenate([seg(2), seg(4), seg(5)], axis=1), 512)
    sh["wao"] = _fm(inp["w_attn_o"][0], 256)
    sh["wgo"] = _fm(inp["w_gla_o"][0], 256)
    sh["wo"] = _fm(inp["w_out"][0], 256)
    wu = inp["w_up"][0]
    sh["wup"] = np.stack([_fm(wu[:, :DFF], 256), _fm(wu[:, DFF:], 256)])
    sh["wdn"] = _fm(inp["w_down"][0], 256)
    sink = np.broadcast_to(inp["attn_sink"][0][None, :], (128, 16))
    sh["smallc"] = np.ascontiguousarray(np.concatenate(
        [_cols(inp["g_mix"][0]), _cols(inp["g_ffn"][0]), inp["q_norm"][0][:, None], inp["k_norm"][0][:, None],
         _cols(inp["gla_norm"][0]), sink], axis=1).astype(f32))
    sh["wg"] = np.stack([inp["w_gate_f"][0], inp["w_gate_b"][0]])
    sh["bg"] = np.stack([inp["b_gate_f"][0][None, :], inp["b_gate_b"][0][None, :]])
    cwt = inp["conv_w"][0]
    sh["convw"] = np.ascontiguousarray(cwt.reshape(3, 88, 128).transpose(2, 1, 0))
    sh["convb"] = _cols(inp["conv_b"][0])
    s = np.arange(128)[:, None]
    t = np.arange(128)[None, :]
    g16 = -1.0 / 16.0
    perm = np.zeros((128, 128), f32)
    perm[(np.arange(128) + 64) % 128, np.arange(128)] = 1.0
    cst = np.stack([(s <= t) * g16, (s >= t) * g16, (s > t) * g16, (s < t) * g16,
                    (s <= t) * 1.0, (s >= t) * 1.0, perm, (t <= s) * 1.0, (s <= t) * 1.0]).astype(f32)
    sh["cst"] = np.ascontiguousarray(cst.transpose(1, 0, 2))
    nfreq = 32
    inv = (10000.0 ** (-np.arange(nfreq, dtype=np.float32) / nfreq)).astype(f32)
    pos = np.arange(SEQ)
    ang = np.concatenate([(pos // 64)[:, None].astype(f32) * inv, (pos % 64)[:, None].astype(f32) * inv], axis=-1)
    cosf, sinf = np.cos(ang).astype(f32), np.sin(ang).astype(f32)
    blob_names = [nm for nm, _ in KB.WSPEC]
    if gather and gather != "dry":
        offs, R = KB.blob_layout()
        flats = [np.zeros(8 * R[bi] * KB.BLOB_C, f32) for bi in range(2)]
        for nm in blob_names:
            bi, o_, n_ = offs[nm]
            flats[bi][o_:o_ + n_] = sh[nm].ravel()
        blobs = [flats[bi].reshape(8, R[bi], KB.BLOB_C) for bi in range(2)]
    percore = []
    for core in range(8):
        b, j = core // 4, core % 4
        t0 = j * T
        X = np.zeros((T + 512, D), f32)
        X[:T] = x[b, t0:t0 + T]
        if j > 0:
            X[T:T + 128] = x[b, t0 - 128:t0]
        if j < 3:
            X[T + 128:T + 256] = x[b, t0 + T:t0 + T + 128]
        X[T + 256:] = ctx[b]
        d = {}
        d["xe"] = np.ascontiguousarray(X.T.reshape(NDC, 128, T + 512).transpose(1, 0, 2))
        d["cc"] = np.ascontiguousarray(np.stack([_cols(c[b]), _cols(c_ctx)], axis=-1))
        C = np.ones((T + 512, 64), f32)
        S = np.zeros((T + 512, 64), f32)
        C[:T], S[:T] = cosf[t0:t0 + T], sinf[t0:t0 + T]
        if j > 0:
            C[T:T + 128], S[T:T + 128] = cosf[t0 - 128:t0], sinf[t0 - 128:t0]
        if j < 3:
            C[T + 128:T + 256], S[T + 128:T + 256] = cosf[t0 + T:t0 + T + 128], sinf[t0 + T:t0 + T + 128]
        d["ropeC"] = np.ascontiguousarray(np.concatenate([C, C], axis=1).T)
        d["ropeS"] = np.ascontiguousarray(np.concatenate([-S, S], axis=1).T)
        ml = np.tile(cst[7], (1, 4)) * (1.0 if j > 0 else 0.0)
        mr = np.tile(cst[8], (1, 4)) * (1.0 if j < 3 else 0.0)
        d["pcm"] = np.ascontiguousarray(np.stack([ml, mr], axis=1).astype(f32))
        sl = np.zeros((128, 32), f32)
        for r in range(8):
            same = (r // 4) == b
            jr = r % 4
            sl[:, r] = 1.0 if (same and jr < j) else 0.0
            sl[:, 8 + r] = 1.0 if (same and jr > j) else 0.0
            sl[:, 16 + r] = 1.0 if (same and jr == j - 1) else 0.0
            sl[:, 24 + r] = 1.0 if (same and jr == j + 1) else 0.0
        d["sel"] = sl
        for k_, v_ in sh.items():
            if gather and k_ in blob_names:
                continue
            if gather == "dry" and k_ in blob_names:
                continue
            d[k_] = v_
        if gather == "dry":
            d["wseed"] = (np.random.default_rng(1).standard_normal((128, 16384)) * (D ** -0.5)).astype(f32)
        if gather and gather != "dry":
            d["wblob0"] = blobs[0][core]
            d["wblob1"] = blobs[1][core]
        percore.append(d)
    return percore


_NC_CACHE = {}


def kernel(**inputs):
    inp = {k: np.asarray(v) for k, v in inputs.items()}
    T = inp["x"].shape[1] // 4
    if T not in _NC_CACHE:
        _NC_CACHE[T] = KB(T).build()
    nc = _NC_CACHE[T]
    in_maps = host_prep(inp, T)
    res = run_bass_kernel_spmd(nc, in_maps, core_ids=list(range(8)))
    out = np.empty((2, 4 * T, D), np.float32)
    for core in range(8):
        b, j = core // 4, core % 4
        o = np.asarray(res.results[core]["outT"])
        out[b, j * T:(j + 1) * T, :] = o.transpose(2, 1, 0).reshape(T, D)
    return out
```
